# Optimizing a Trainium2 kernel written in Bass

```python
import jax, jax.numpy as jnp
from jax import lax
import numpy as np

D_MODEL = 1024
BATCH = 1
SEQ = 16384
DEPTH = 4

N_MIXERS = 2
N_SSM_LAYERS = (DEPTH + 1) // 2
N_ATTN_LAYERS = DEPTH // 2
SSM_WIDTH = D_MODEL
SSM_GROUP = 16
SSM_GROUPS = SSM_WIDTH // SSM_GROUP
SSM_STATE = 64
DT_MIN = 1e-3
DT_MAX = 1e-1
HEAD_DIM = 128
N_HEADS = D_MODEL // HEAD_DIM
ATTN_WIDTH = N_HEADS * HEAD_DIM
ROT_DIM = HEAD_DIM // 4
ROPE_THETA = 500000.0
MOBA_BLOCK = 256
MOBA_TOPK = 3
Q_CHUNK = 128
NORM_EPS = 1e-6

kernel_name = "hybrid_s5_moba_interleaved"


def rms_norm(x, g):
    xf = x.astype(jnp.float32)
    y = xf * lax.rsqrt(jnp.mean(xf * xf, axis=-1, keepdims=True) + NORM_EPS)
    return (y * g.astype(jnp.float32)).astype(x.dtype)


def partial_rotary(x, pos):
    half = ROT_DIM // 2
    inv_freq = ROPE_THETA ** (-(jnp.arange(half, dtype=jnp.float32) * 2.0) / ROT_DIM)
    ang = pos.astype(jnp.float32)[:, None] * inv_freq[None, :]
    cos, sin = jnp.cos(ang), jnp.sin(ang)
    xf = x.astype(jnp.float32)
    x1, x2, rest = xf[..., :half], xf[..., half:ROT_DIM], xf[..., ROT_DIM:]
    out = jnp.concatenate([x1 * cos - x2 * sin, x2 * cos + x1 * sin, rest], axis=-1)
    return out.astype(x.dtype)


def _complex_linear_combine(e1, e2):
    a1r, a1i, b1r, b1i = e1
    a2r, a2i, b2r, b2i = e2
    ar = a2r * a1r - a2i * a1i
    ai = a2r * a1i + a2i * a1r
    br = a2r * b1r - a2i * b1i + b2r
    bi = a2r * b1i + a2i * b1r + b2i
    return (ar, ai, br, bi)


def s5_mixer(h, w_in, a_re, a_im, log_dt, b_re, b_im, c_re, c_im, d_skip, w_glu, b_glu, w_out):
    bsz, seq, _ = h.shape
    proj = h @ w_in
    u, z = proj[..., :SSM_WIDTH], proj[..., SSM_WIDTH:]
    uf = u.astype(jnp.float32)
    ug = uf.reshape(bsz, seq, SSM_GROUPS, SSM_GROUP)
    dt = jnp.exp(log_dt.astype(jnp.float32))[:, None]
    lr, li = a_re.astype(jnp.float32), a_im.astype(jnp.float32)
    mag = jnp.exp(lr * dt)
    ab_re, ab_im = mag * jnp.cos(li * dt), mag * jnp.sin(li * dt)
    den = lr * lr + li * li
    nr, ni = ab_re - 1.0, ab_im
    f_re = (nr * lr + ni * li) / den
    f_im = (ni * lr - nr * li) / den
    bu_re = jnp.einsum('blgc,gpc->blgp', ug, b_re.astype(jnp.float32))
    bu_im = jnp.einsum('blgc,gpc->blgp', ug, b_im.astype(jnp.float32))
    in_re = f_re * bu_re - f_im * bu_im
    in_im = f_re * bu_im + f_im * bu_re
    a_re_t = jnp.broadcast_to(ab_re, in_re.shape)
    a_im_t = jnp.broadcast_to(ab_im, in_re.shape)
    _, _, s_re, s_im = lax.associative_scan(
        _complex_linear_combine, (a_re_t, a_im_t, in_re, in_im), axis=1)
    y = (jnp.einsum('blgp,gcp->blgc', s_re, c_re.astype(jnp.float32))
         - jnp.einsum('blgp,gcp->blgc', s_im, c_im.astype(jnp.float32)))
    y = y.reshape(bsz, seq, SSM_WIDTH) + d_skip.astype(jnp.float32) * uf
    y = jax.nn.gelu(y)
    y = y * jax.nn.sigmoid(y @ w_glu.astype(jnp.float32) + b_glu.astype(jnp.float32))
    y = y.astype(h.dtype) * jax.nn.silu(z)
    return y @ w_out


def moba_mixer(h, w_in, q_gain, k_gain, w_out):
    bsz, seq, _ = h.shape
    proj = h @ w_in
    q, k, v, z = jnp.split(proj, 4, axis=-1)

    def heads(t):
        return t.reshape(bsz, seq, N_HEADS, HEAD_DIM).transpose(0, 2, 1, 3)

    pos = jnp.arange(seq)
    q = partial_rotary(rms_norm(heads(q), q_gain), pos)
    k = partial_rotary(rms_norm(heads(k), k_gain), pos)
    v = heads(v)
    n_blocks = -(-seq // MOBA_BLOCK)
    pad = n_blocks * MOBA_BLOCK - seq
    k_pad = jnp.pad(k, ((0, 0), (0, 0), (0, pad), (0, 0)))
    v_pad = jnp.pad(v, ((0, 0), (0, 0), (0, pad), (0, 0)))
    k_blocks = k_pad.reshape(bsz, N_HEADS, n_blocks, MOBA_BLOCK, HEAD_DIM)
    v_blocks = v_pad.reshape(bsz, N_HEADS, n_blocks, MOBA_BLOCK, HEAD_DIM)
    k_mean = jnp.mean(k_blocks.astype(jnp.float32), axis=3)
    top_k = min(MOBA_TOPK, n_blocks)
    scale = HEAD_DIM ** -0.5
    b_idx = jnp.arange(bsz)[:, None, None, None]
    h_idx = jnp.arange(N_HEADS)[None, :, None, None]
    blk_ids = jnp.arange(n_blocks)
    key_off = jnp.arange(MOBA_BLOCK)

    def chunk(c):
        start = c * Q_CHUNK
        qc = lax.dynamic_slice_in_dim(q, start, Q_CHUNK, axis=2).astype(jnp.float32)
        own = start // MOBA_BLOCK
        q_pos = start + jnp.arange(Q_CHUNK)
        gate = jnp.einsum('bhqd,bhnd->bhqn', qc, k_mean)
        gate = jnp.where(blk_ids[None, None, None, :] < own, gate, -jnp.inf)
        _, sel = lax.top_k(gate, top_k)
        sel_valid = sel < own
        k_sel = k_blocks[b_idx, h_idx, sel].astype(jnp.float32)
        v_sel = v_blocks[b_idx, h_idx, sel].astype(jnp.float32)
        s_sel = jnp.einsum('bhqd,bhqnkd->bhqnk', qc, k_sel) * scale
        s_sel = jnp.where(sel_valid[..., None], s_sel, -jnp.inf)
        k_own = lax.dynamic_slice_in_dim(k_pad, own * MOBA_BLOCK, MOBA_BLOCK, axis=2).astype(jnp.float32)
        v_own = lax.dynamic_slice_in_dim(v_pad, own * MOBA_BLOCK, MOBA_BLOCK, axis=2).astype(jnp.float32)
        s_own = jnp.einsum('bhqd,bhkd->bhqk', qc, k_own) * scale
        causal = (own * MOBA_BLOCK + key_off)[None, :] <= q_pos[:, None]
        s_own = jnp.where(causal, s_own, -jnp.inf)
        logits = jnp.concatenate(
            [s_own, s_sel.reshape(bsz, N_HEADS, Q_CHUNK, top_k * MOBA_BLOCK)], axis=-1)
        p = jax.nn.softmax(logits, axis=-1)
        p_own = p[..., :MOBA_BLOCK]
        p_sel = p[..., MOBA_BLOCK:].reshape(bsz, N_HEADS, Q_CHUNK, top_k, MOBA_BLOCK)
        o = (jnp.einsum('bhqk,bhkd->bhqd', p_own, v_own)
             + jnp.einsum('bhqnk,bhqnkd->bhqd', p_sel, v_sel))
        return o.astype(h.dtype)

    out = lax.map(chunk, jnp.arange(seq // Q_CHUNK))
    out = out.transpose(1, 0, 3, 2, 4).reshape(bsz, seq, ATTN_WIDTH)
    return (out * jax.nn.silu(z)) @ w_out


def setup_inputs(seed: int = 0) -> dict:
    key = jax.random.key(seed)
    ks = jax.random.split(key, 20)
    f32 = jnp.float32
    na, nb = N_SSM_LAYERS, N_ATTN_LAYERS
    G, P, C = SSM_GROUPS, SSM_STATE, SSM_GROUP
    x = jax.random.normal(ks[0], (BATCH, SEQ, D_MODEL), f32)
    norm_g = 1.0 + 0.02 * jax.random.normal(ks[1], (DEPTH, D_MODEL), f32)
    ssm_w_in = jax.random.normal(ks[2], (na, D_MODEL, 2 * SSM_WIDTH), f32) * D_MODEL ** -0.5
    ssm_a_re = -0.5 + 0.01 * jax.random.normal(ks[3], (na, G, P), f32)
    ssm_a_im = (jnp.pi * jnp.arange(P, dtype=f32))[None, None, :] + 0.01 * jax.random.normal(ks[4], (na, G, P), f32)
    ssm_log_dt = jax.random.uniform(ks[5], (na, G), f32, minval=float(np.log(DT_MIN)), maxval=float(np.log(DT_MAX)))
    ssm_b_re = jax.random.normal(ks[6], (na, G, P, C), f32) * (2 * C) ** -0.5
    ssm_b_im = jax.random.normal(ks[7], (na, G, P, C), f32) * (2 * C) ** -0.5
    ssm_c_re = jax.random.normal(ks[8], (na, G, C, P), f32) * P ** -0.5
    ssm_c_im = jax.random.normal(ks[9], (na, G, C, P), f32) * P ** -0.5
    ssm_d = jax.random.normal(ks[10], (na, SSM_WIDTH), f32)
    ssm_w_glu = jax.random.normal(ks[11], (na, SSM_WIDTH, SSM_WIDTH), f32) * SSM_WIDTH ** -0.5
    ssm_b_glu = 0.01 * jax.random.normal(ks[12], (na, SSM_WIDTH), f32)
    ssm_w_out = jax.random.normal(ks[13], (na, SSM_WIDTH, D_MODEL), f32) * SSM_WIDTH ** -0.5
    attn_w_in = jax.random.normal(ks[14], (nb, D_MODEL, 4 * ATTN_WIDTH), f32) * D_MODEL ** -0.5
    attn_q_gain = 1.0 + 0.02 * jax.random.normal(ks[15], (nb, HEAD_DIM), f32)
    attn_k_gain = 1.0 + 0.02 * jax.random.normal(ks[16], (nb, HEAD_DIM), f32)
    attn_w_out = jax.random.normal(ks[17], (nb, ATTN_WIDTH, D_MODEL), f32) * ATTN_WIDTH ** -0.5
    return {"x": x, "norm_g": norm_g, "ssm_w_in": ssm_w_in, "ssm_a_re": ssm_a_re,
            "ssm_a_im": ssm_a_im, "ssm_log_dt": ssm_log_dt, "ssm_b_re": ssm_b_re,
            "ssm_b_im": ssm_b_im, "ssm_c_re": ssm_c_re, "ssm_c_im": ssm_c_im,
            "ssm_d": ssm_d, "ssm_w_glu": ssm_w_glu, "ssm_b_glu": ssm_b_glu,
            "ssm_w_out": ssm_w_out, "attn_w_in": attn_w_in, "attn_q_gain": attn_q_gain,
            "attn_k_gain": attn_k_gain, "attn_w_out": attn_w_out}


def reference(x, norm_g, ssm_w_in, ssm_a_re, ssm_a_im, ssm_log_dt, ssm_b_re, ssm_b_im,
              ssm_c_re, ssm_c_im, ssm_d, ssm_w_glu, ssm_b_glu, ssm_w_out,
              attn_w_in, attn_q_gain, attn_k_gain, attn_w_out):
    h = x
    for i in range(DEPTH):
        hn = rms_norm(h, norm_g[i])
        j = i // N_MIXERS
        if i % N_MIXERS == 0:
            y = s5_mixer(hn, ssm_w_in[j], ssm_a_re[j], ssm_a_im[j], ssm_log_dt[j],
                         ssm_b_re[j], ssm_b_im[j], ssm_c_re[j], ssm_c_im[j], ssm_d[j],
                         ssm_w_glu[j], ssm_b_glu[j], ssm_w_out[j])
        else:
            y = moba_mixer(hn, attn_w_in[j], attn_q_gain[j], attn_k_gain[j], attn_w_out[j])
        h = h + y
    return h
```

```python
import contextlib
import math
import numpy as np
import ml_dtypes
import concourse.bass as bass
import concourse.mybir as mybir
from concourse.bass_utils import run_bass_kernel_spmd

F32 = mybir.dt.float32
BF16 = mybir.dt.bfloat16
ALU = mybir.AluOpType
AF = mybir.ActivationFunctionType
AX = mybir.AxisListType
NPBF = ml_dtypes.bfloat16

NCORES = 8
SEQ = 16384
D = 1024
TOK = SEQ // NCORES
NT = TOK // 128
EPS = 1e-6
ROPE_THETA = 500000.0

ENGS = ("pe", "act", "dve", "pool", "sp")
N_DMA_SEMS = 10


class Op:
    __slots__ = ("eng", "fn", "deps", "is_dma", "ticket", "dsem", "dval", "need_sig", "prewait")

    def __init__(self, eng, fn, is_dma):
        self.eng = eng
        self.fn = fn
        self.is_dma = is_dma
        self.deps = []
        self.ticket = None
        self.dsem = None
        self.dval = None
        self.need_sig = False
        self.prewait = None


class Sched:
    def __init__(self, nc, stack):
        self.nc = nc
        self.ops = {e: [] for e in ENGS}
        self.last_w = {}
        self.readers = {}
        self.esem = {e: stack.enter_context(nc.semaphore("es_" + e)) for e in ENGS}
        self.dsems = {e: [stack.enter_context(nc.semaphore("ds_%s_%d" % (e, i)))
                          for i in range(N_DMA_SEMS)] for e in ("sp", "act", "pool")}
        self.dcount = {e: [0] * N_DMA_SEMS for e in self.dsems}
        self.dlast = {e: [None] * N_DMA_SEMS for e in self.dsems}
        self.drr = {e: 0 for e in self.dsems}
        self.all_ops = []
        self.out_dmas = []

    def _add(self, eng, fn, reads, writes, is_dma):
        op = Op(eng, fn, is_dma)
        deps = set()
        for r in reads:
            w = self.last_w.get(r)
            if w is not None:
                deps.add(w)
        for w_ in writes:
            w = self.last_w.get(w_)
            if w is not None:
                deps.add(w)
            for rd in self.readers.get(w_, ()):
                deps.add(rd)
        for r in reads:
            self.readers.setdefault(r, []).append(op)
        for w_ in writes:
            self.last_w[w_] = op
            self.readers[w_] = []
        op.deps = list(deps)
        if is_dma:
            i = self.drr[eng]
            self.drr[eng] = (i + 1) % N_DMA_SEMS
            self.dcount[eng][i] += 16
            op.dsem = self.dsems[eng][i]
            op.dval = self.dcount[eng][i]
            op.prewait = self.dlast[eng][i]
            self.dlast[eng][i] = op
        self.ops[eng].append(op)
        self.all_ops.append(op)
        return op

    def op(self, eng, fn, reads=(), writes=()):
        return self._add(eng, fn, reads, writes, False)

    def dma(self, eng, fn, reads=(), writes=(), is_out=False):
        o = self._add(eng, fn, reads, writes, True)
        if is_out:
            self.out_dmas.append(o)
        return o

    def emit(self):
        nc = self.nc
        for op in self.all_ops:
            for d in op.deps:
                if d.is_dma:
                    continue
                if d.eng == op.eng and d.eng == "pe" and not op.is_dma:
                    continue
                d.need_sig = True
        for e in ENGS:
            t = 0
            for op in self.ops[e]:
                if op.is_dma or not op.need_sig:
                    continue
                t += 1
                op.ticket = t
        esem = self.esem
        final_waits = self.out_dmas

        def emit_engine(ename, eng):
            waited = {e: 0 for e in ENGS}
            dwaited = {}
            for op in self.ops[ename]:
                waits = {}
                dw = {}
                deps = list(op.deps)
                if op.prewait is not None:
                    deps.append(op.prewait)
                for d in deps:
                    if d.is_dma:
                        k = d.dsem
                        if dwaited.get(k, 0) < d.dval:
                            dw[k] = max(dw.get(k, 0), d.dval)
                    else:
                        if d.eng == ename and ename == "pe" and not op.is_dma:
                            continue
                        if waited[d.eng] < d.ticket:
                            waits[d.eng] = max(waits.get(d.eng, 0), d.ticket)
                for e_, v in waits.items():
                    eng.wait_ge(esem[e_], v)
                    waited[e_] = v
                for k, v in dw.items():
                    eng.wait_ge(k, v)
                    dwaited[k] = v
                ins = op.fn(eng)
                if op.is_dma:
                    ins.then_inc(op.dsem, 16)
                elif op.need_sig:
                    ins.then_inc(esem[ename], 1)
            if ename == "sp":
                for o in final_waits:
                    if dwaited.get(o.dsem, 0) < o.dval:
                        eng.wait_ge(o.dsem, o.dval)
                        dwaited[o.dsem] = o.dval

        with nc.Block() as block:
            @block.tensor
            def _(eng):
                emit_engine("pe", eng)

            @block.scalar
            def _(eng):
                emit_engine("act", eng)

            @block.vector
            def _(eng):
                emit_engine("dve", eng)

            @block.gpsimd
            def _(eng):
                emit_engine("pool", eng)

            @block.sync
            def _(eng):
                emit_engine("sp", eng)


class Ctx:
    def __init__(self):
        self.nc = bass.Bass("TRN2", target_bir_lowering=False)
        self.st = contextlib.ExitStack()
        self.S = Sched(self.nc, self.st)
        self._n = 0

    def din(self, name, shape, dt):
        return self.nc.dram_tensor(name, list(shape), dt, kind="ExternalInput").ap()

    def dout(self, name, shape, dt):
        return self.nc.dram_tensor(name, list(shape), dt, kind="ExternalOutput").ap()

    def sb(self, name, shape, dt):
        return self.st.enter_context(self.nc.sbuf_tensor("sb_" + name, list(shape), dt))

    def ps(self, name, shape, dt=F32):
        return self.st.enter_context(self.nc.psum_tensor("ps_" + name, list(shape), dt))

    def finish(self):
        self.S.emit()
        self.st.close()
        return self.nc


def copy_on(S, eng, out, in_, reads, writes):
    if eng == "act":
        return S.op("act", lambda e: e.copy(out=out, in_=in_), reads=reads, writes=writes)
    return S.op(eng, lambda e: e.tensor_copy(out=out, in_=in_), reads=reads, writes=writes)


def emit_load_w(C, w_dram, dst, ncols, key, cast_engs=("act", "pool")):
    S = C.S
    if not hasattr(C, "wstg"):
        C.wstg = [C.sb("wstg%d" % i, [128, 8, 256], F32) for i in range(2)]
        C.wstg_n = 0
    wv = w_dram.rearrange("(k p) n -> p k n", p=128)
    for cb in range(ncols // 256):
        i = C.wstg_n % 2
        C.wstg_n += 1
        sg = C.wstg[i]
        skey = ("wstg", i)
        S.dma("sp", lambda e, sg=sg, cb=cb: e.dma_start(out=sg[:], in_=wv[:, :, cb * 256:(cb + 1) * 256]),
              writes=[skey])
        copy_on(S, cast_engs[(cb // 2) % len(cast_engs)], dst[:, :, cb * 256:(cb + 1) * 256], sg[:],
                reads=[skey], writes=[(key, cb // 2)])


def emit_norm_T(C, h_dram, g_rep, hnT, ident_bf):
    S = C.S
    hb = [C.sb("n_hb%d" % i, [128, D], F32) for i in range(2)]
    junk = C.sb("n_junk", [128, D], F32)
    ss = [C.sb("n_ss%d" % i, [128, 1], F32) for i in range(2)]
    rstd = [C.sb("n_rstd%d" % i, [128, 1], F32) for i in range(2)]
    hn = [C.sb("n_hn%d" % i, [128, D], BF16) for i in range(2)]
    pT = [C.ps("n_pT%d" % i, [128, D], BF16) for i in range(2)]
    epsb = C.sb("epsb", [128, 1], F32)
    C.epsb = epsb
    S.op("pool", lambda e: e.memset(epsb[:], EPS), writes=["epsb"])
    for t in range(NT):
        i = t % 2
        S.dma("sp", lambda e, i=i, t=t: e.dma_start(out=hb[i][:], in_=h_dram[t * 128:(t + 1) * 128, :]),
              writes=[("n_hb", i)])
        S.op("act", lambda e, i=i: e.activation(out=junk[:], in_=hb[i][:], func=AF.Square, accum_out=ss[i][:]),
             reads=[("n_hb", i)], writes=["n_junk", ("n_ss", i)])
        S.op("act", lambda e, i=i: e.activation(out=rstd[i][:], in_=ss[i][:], func=AF.Sqrt, scale=1.0 / D, bias=epsb[:, 0:1]),
             reads=[("n_ss", i), "epsb"], writes=[("n_rstd", i)])
        S.op("dve", lambda e, i=i: e.reciprocal(out=rstd[i][:], in_=rstd[i][:]),
             reads=[("n_rstd", i)], writes=[("n_rstd", i)])
        S.op("dve", lambda e, i=i: e.scalar_tensor_tensor(out=hn[i][:], in0=hb[i][:], scalar=rstd[i][:, 0:1],
                                                          in1=g_rep[:], op0=ALU.mult, op1=ALU.mult),
             reads=[("n_hb", i), ("n_rstd", i), "g_rep"], writes=[("n_hn", i)])
        for k in range(8):
            S.op("pe", lambda e, i=i, k=k: e.transpose(out=pT[i][:, k * 128:(k + 1) * 128],
                                                       in_=hn[i][:, k * 128:(k + 1) * 128], identity=ident_bf[:]),
                 reads=[("n_hn", i), "ident_bf"], writes=[("n_pT", i)])
        copy_on(S, "act" if t % 2 == 0 else "pool" if False else "dve",
                hnT[:, :, t * 128:(t + 1) * 128], pT[i][:].rearrange("p (k c) -> p k c", k=8),
                reads=[("n_pT", i)], writes=[("hnT", t)])


def emit_outproj(C, y2T, y2keys, Wob, h_dram, hout_dram, ps_list, pskeys):
    S = C.S
    hb = [C.sb("o_hb%d" % i, [128, D], F32) for i in range(2)]
    ho = [C.sb("o_ho%d" % i, [128, D], F32) for i in range(2)]
    n = 0
    for t in range(NT):
        i = t % 2
        S.dma("sp", lambda e, i=i, t=t: e.dma_start(out=hb[i][:], in_=h_dram[t * 128:(t + 1) * 128, :]),
              writes=[("o_hb", i)])
        for cg in range(2):
            ps = ps_list[n % len(ps_list)]
            pk = pskeys[n % len(ps_list)]
            n += 1
            for k in range(8):
                S.op("pe", lambda e, ps=ps, k=k, t=t, cg=cg: e.matmul(
                    ps[:], lhsT=y2T[:, k, t * 128:(t + 1) * 128], rhs=Wob[:, k, cg * 512:(cg + 1) * 512],
                    start=(k == 0), stop=(k == 7)),
                    reads=list(y2keys(k, t)) + [("wo", cg)], writes=[pk])
            S.op("dve", lambda e, ps=ps, i=i, cg=cg: e.tensor_tensor(
                out=ho[i][:, cg * 512:(cg + 1) * 512], in0=ps[:], in1=hb[i][:, cg * 512:(cg + 1) * 512], op=ALU.add),
                reads=[pk, ("o_hb", i)], writes=[("o_ho", i, cg)])
        S.dma("sp", lambda e, i=i, t=t: e.dma_start(out=hout_dram[t * 128:(t + 1) * 128, :], in_=ho[i][:]),
              reads=[("o_ho", i, 0), ("o_ho", i, 1)], is_out=True)


def build_ssm_A():
    C = Ctx()
    S = C.S
    h = C.din("h", [TOK, D], F32)
    g = C.din("g_rep", [128, D], F32)
    w = C.din("w_in", [D, 2 * D], F32)
    idb = C.din("ident_bf", [128, 128], BF16)
    uT = C.dout("uT", [D, TOK], BF16)
    zsT = C.dout("zsT", [D, TOK], BF16)
    g_rep = C.sb("g_rep", [128, D], F32)
    ident_bf = C.sb("ident_bf", [128, 128], BF16)
    hnT = C.sb("hnT", [128, 8, TOK], BF16)
    Wb = C.sb("Wb", [128, 8, 2 * D], BF16)
    S.dma("sp", lambda e: e.dma_start(out=g_rep[:], in_=g), writes=["g_rep"])
    S.dma("sp", lambda e: e.dma_start(out=ident_bf[:], in_=idb), writes=["ident_bf"])
    emit_norm_T(C, h, g_rep, hnT, ident_bf)
    emit_load_w(C, w, Wb, 2 * D, "w")
    pss = [C.ps("a_ps%d" % i, [128, 512], F32) for i in range(3)]
    ost = [C.sb("a_ost%d" % i, [128, 512], BF16) for i in range(3)]
    n = 0
    for f in range(16):
        for tb in range(4):
            j = n % 3
            n += 1
            for k in range(8):
                S.op("pe", lambda e, j=j, k=k, f=f, tb=tb: e.matmul(
                    pss[j][:], lhsT=Wb[:, k, f * 128:(f + 1) * 128], rhs=hnT[:, k, tb * 512:(tb + 1) * 512],
                    start=(k == 0), stop=(k == 7)),
                    reads=[("w", f // 4)] + [("hnT", tb * 4 + x) for x in range(4)], writes=[("a_ps", j)])
            if f < 8:
                copy_on(S, "dve", ost[j][:], pss[j][:], reads=[("a_ps", j)], writes=[("a_ost", j)])
                dst = uT[f * 128:(f + 1) * 128, tb * 512:(tb + 1) * 512]
            else:
                S.op("act", lambda e, j=j: e.activation(out=ost[j][:], in_=pss[j][:], func=AF.Silu),
                     reads=[("a_ps", j)], writes=[("a_ost", j)])
                dst = zsT[(f - 8) * 128:(f - 7) * 128, tb * 512:(tb + 1) * 512]
            S.dma("sp", lambda e, j=j, dst=dst: e.dma_start(out=dst, in_=ost[j][:]),
                  reads=[("a_ost", j)], is_out=True)
    return C.finish()


def build_ssm_C():
    C = Ctx()
    S = C.S
    h = C.din("h", [TOK, D], F32)
    yTd = C.din("yT", [D, TOK], BF16)
    zsTd = C.din("zsT", [D, TOK], BF16)
    wg = C.din("w_glu", [D, D], F32)
    bg = C.din("b_glu", [128, 8], F32)
    wo = C.din("w_out", [D, D], F32)
    hout = C.dout("hout", [TOK, D], F32)
    yT = C.sb("yT", [128, 8, TOK], BF16)
    zsT = C.sb("zsT", [128, 8, TOK], BF16)
    y2T = C.sb("y2T", [128, 8, TOK], BF16)
    Wgb = C.sb("Wgb", [128, 8, D], BF16)
    Wob = C.sb("Wob", [128, 8, D], BF16)
    bgl = C.sb("bgl", [128, 8], F32)
    S.dma("sp", lambda e: e.dma_start(out=bgl[:], in_=bg), writes=["bgl"])
    for k in range(8):
        S.dma("sp", lambda e, k=k: e.dma_start(out=yT[:, k, :], in_=yTd[k * 128:(k + 1) * 128, :]),
              writes=[("yT", k)])
        S.dma("sp", lambda e, k=k: e.dma_start(out=zsT[:, k, :], in_=zsTd[k * 128:(k + 1) * 128, :]),
              writes=[("zsT", k)])
    emit_load_w(C, wg, Wgb, D, "wg")
    emit_load_w(C, wo, Wob, D, "wo")
    pss = [C.ps("c_ps%d" % i, [128, 512], F32) for i in range(3)]
    sg = [C.sb("c_sg%d" % i, [128, 512], BF16) for i in range(3)]
    tmp = [C.sb("c_tmp%d" % i, [128, 512], BF16) for i in range(3)]
    n = 0
    for f in range(8):
        for tb in range(4):
            j = n % 3
            n += 1
            for k in range(8):
                S.op("pe", lambda e, j=j, k=k, f=f, tb=tb: e.matmul(
                    pss[j][:], lhsT=Wgb[:, k, f * 128:(f + 1) * 128], rhs=yT[:, k, tb * 512:(tb + 1) * 512],
                    start=(k == 0), stop=(k == 7)),
                    reads=[("wg", f // 4), ("yT", k)], writes=[("c_ps", j)])
            S.op("act", lambda e, j=j, f=f: e.activation(out=sg[j][:], in_=pss[j][:], func=AF.Sigmoid,
                                                         bias=bgl[:, f:f + 1]),
                 reads=[("c_ps", j), "bgl"], writes=[("c_sg", j)])
            S.op("dve", lambda e, j=j, f=f, tb=tb: e.tensor_tensor(
                out=tmp[j][:], in0=yT[:, f, tb * 512:(tb + 1) * 512], in1=sg[j][:], op=ALU.mult),
                reads=[("c_sg", j), ("yT", f)], writes=[("c_tmp", j)])
            S.op("pool", lambda e, j=j, f=f, tb=tb: e.tensor_tensor(
                out=y2T[:, f, tb * 512:(tb + 1) * 512], in0=tmp[j][:], in1=zsT[:, f, tb * 512:(tb + 1) * 512],
                op=ALU.mult),
                reads=[("c_tmp", j), ("zsT", f)], writes=[("y2T", f, tb)])
    emit_outproj(C, y2T, lambda k, t: [("y2T", k, t // 4)], Wob, h, hout, pss,
                 [("c_ps", i) for i in range(3)])
    return C.finish()


def build_attn_C():
    C = Ctx()
    S = C.S
    h = C.din("h", [TOK, D], F32)
    oTd = C.din("oT", [D, TOK], BF16)
    zsTd = C.din("zsT", [D, TOK], BF16)
    wo = C.din("w_out", [D, D], F32)
    hout = C.dout("hout", [TOK, D], F32)
    oT = C.sb("oT", [128, 8, TOK], BF16)
    zsT = C.sb("zsT", [128, 8, TOK], BF16)
    y2T = C.sb("y2T", [128, 8, TOK], BF16)
    Wob = C.sb("Wob", [128, 8, D], BF16)
    for k in range(8):
        S.dma("sp", lambda e, k=k: e.dma_start(out=oT[:, k, :], in_=oTd[k * 128:(k + 1) * 128, :]),
              writes=[("oT", k)])
        S.dma("sp", lambda e, k=k: e.dma_start(out=zsT[:, k, :], in_=zsTd[k * 128:(k + 1) * 128, :]),
              writes=[("zsT", k)])
    emit_load_w(C, wo, Wob, D, "wo")
    for k in range(8):
        S.op("dve" if k % 2 == 0 else "pool", lambda e, k=k: e.tensor_tensor(
            out=y2T[:, k, :], in0=oT[:, k, :], in1=zsT[:, k, :], op=ALU.mult),
            reads=[("oT", k), ("zsT", k)], writes=[("y2T", k)])
    pss = [C.ps("c_ps%d" % i, [128, 512], F32) for i in range(3)]
    emit_outproj(C, y2T, lambda k, t: [("y2T", k)], Wob, h, hout, pss, [("c_ps", i) for i in range(3)])
    return C.finish()


def build_attn_A():
    C = Ctx()
    S = C.S
    h = C.din("h", [TOK, D], F32)
    g = C.din("g_rep", [128, D], F32)
    w = C.din("w_in", [D, 4 * D], F32)
    idb = C.din("ident_bf", [128, 128], BF16)
    qg = C.din("qg_rep", [128, 128], F32)
    kg = C.din("kg_rep", [128, 128], F32)
    cosd = C.din("cos4", [TOK, 64], F32)
    sind = C.din("sin4", [TOK, 64], F32)
    qo = C.dout("q", [TOK, D], BF16)
    ko = C.dout("k", [TOK, D], BF16)
    vo = C.dout("v", [TOK, D], BF16)
    zsT = C.dout("zsT", [D, TOK], BF16)
    g_rep = C.sb("g_rep", [128, D], F32)
    ident_bf = C.sb("ident_bf", [128, 128], BF16)
    gains = [C.sb("qg", [128, 128], F32), C.sb("kg", [128, 128], F32)]
    cos = C.sb("cos", [128, NT, 64], F32)
    sin = C.sb("sin", [128, NT, 64], F32)
    hnT = C.sb("hnT", [128, 8, TOK], BF16)
    Wb = C.sb("Wb", [128, 8, 4 * D], BF16)
    S.dma("sp", lambda e: e.dma_start(out=g_rep[:], in_=g), writes=["g_rep"])
    S.dma("sp", lambda e: e.dma_start(out=ident_bf[:], in_=idb), writes=["ident_bf"])
    S.dma("sp", lambda e: e.dma_start(out=gains[0][:], in_=qg), writes=["gains"])
    S.dma("sp", lambda e: e.dma_start(out=gains[1][:], in_=kg), writes=["gains"])
    S.dma("sp", lambda e: e.dma_start(out=cos[:], in_=cosd.rearrange("(t p) c -> p t c", p=128)), writes=["cos"])
    S.dma("sp", lambda e: e.dma_start(out=sin[:], in_=sind.rearrange("(t p) c -> p t c", p=128)), writes=["sin"])
    emit_norm_T(C, h, g_rep, hnT, ident_bf)
    emit_load_w(C, w, Wb, 4 * D, "w")
    pss = [C.ps("a_ps%d" % i, [128, 512], F32) for i in range(3)]
    ost = [C.sb("a_ost%d" % i, [128, 512], BF16) for i in range(3)]
    sq = [C.sb("a_sq%d" % i, [128, 512], F32) for i in range(2)]
    xn = [C.sb("a_xn%d" % i, [128, 512], F32) for i in range(2)]
    ss4 = [C.sb("a_ss%d" % i, [128, 4], F32) for i in range(2)]
    rt = [C.sb("a_rt%d" % i, [128, 4, 4, 16], F32) for i in range(2)]
    n = 0
    m = 0
    for t in range(NT):
        for cg in range(6):
            j = n % 3
            n += 1
            for k in range(8):
                S.op("pe", lambda e, j=j, k=k, t=t, cg=cg: e.matmul(
                    pss[j][:], lhsT=hnT[:, k, t * 128:(t + 1) * 128], rhs=Wb[:, k, cg * 512:(cg + 1) * 512],
                    start=(k == 0), stop=(k == 7)),
                    reads=[("w", cg), ("hnT", t)], writes=[("a_ps", j)])
            if cg >= 4:
                copy_on(S, "act", ost[j][:], pss[j][:], reads=[("a_ps", j)], writes=[("a_ost", j)])
                dst = vo[t * 128:(t + 1) * 128, (cg - 4) * 512:(cg - 3) * 512]
            else:
                i = m % 2
                m += 1
                gain = gains[cg // 2]
                S.op("act", lambda e, i=i, j=j: e.activation(out=sq[i][:], in_=pss[j][:], func=AF.Square),
                     reads=[("a_ps", j)], writes=[("a_sq", i)])
                S.op("dve", lambda e, i=i: e.tensor_reduce(out=ss4[i][:], in_=sq[i][:].rearrange("p (a b) -> p a b", a=4),
                                                           axis=AX.X, op=ALU.add),
                     reads=[("a_sq", i)], writes=[("a_ss", i)])
                S.op("act", lambda e, i=i: e.activation(out=ss4[i][:], in_=ss4[i][:], func=AF.Sqrt, scale=1.0 / 128,
                                                        bias=C.epsb[:, 0:1]),
                     reads=[("a_ss", i), "epsb"], writes=[("a_ss", i)])
                S.op("dve", lambda e, i=i: e.reciprocal(out=ss4[i][:], in_=ss4[i][:]),
                     reads=[("a_ss", i)], writes=[("a_ss", i)])
                for hh in range(4):
                    S.op("dve", lambda e, i=i, j=j, hh=hh, gain=gain: e.scalar_tensor_tensor(
                        out=xn[i][:, hh * 128:(hh + 1) * 128], in0=pss[j][:, hh * 128:(hh + 1) * 128],
                        scalar=ss4[i][:, hh:hh + 1], in1=gain[:], op0=ALU.mult, op1=ALU.mult),
                        reads=[("a_ps", j), ("a_ss", i), "gains"], writes=[("a_xn", i, hh)])
                xkeys = [("a_xn", i, hh) for hh in range(4)]
                copy_on(S, "act", ost[j][:], xn[i][:], reads=xkeys, writes=[("a_ost", j)])
                xn3 = xn[i][:].rearrange("p (a b) -> p a b", a=4)
                x1 = xn3[:, :, 0:16]
                x2 = xn3[:, :, 16:32]
                cc = cos[:, t, :].rearrange("p (a b) -> p a b", a=4)
                sn = sin[:, t, :].rearrange("p (a b) -> p a b", a=4)
                o3 = ost[j][:].rearrange("p (a b) -> p a b", a=4)
                for q_, (a_, b_) in enumerate(((x1, cc), (x2, sn), (x2, cc), (x1, sn))):
                    S.op("pool", lambda e, i=i, q_=q_, a_=a_, b_=b_: e.tensor_tensor(
                        out=rt[i][:, :, q_, :], in0=a_, in1=b_, op=ALU.mult),
                        reads=xkeys + ["cos", "sin"], writes=[("a_rt", i, q_)])
                S.op("pool", lambda e, i=i, o3=o3: e.tensor_tensor(
                    out=o3[:, :, 0:16], in0=rt[i][:, :, 0, :], in1=rt[i][:, :, 1, :], op=ALU.subtract),
                    reads=[("a_rt", i, 0), ("a_rt", i, 1)], writes=[("a_ost", j)])
                S.op("pool", lambda e, i=i, o3=o3: e.tensor_tensor(
                    out=o3[:, :, 16:32], in0=rt[i][:, :, 2, :], in1=rt[i][:, :, 3, :], op=ALU.add),
                    reads=[("a_rt", i, 2), ("a_rt", i, 3)], writes=[("a_ost", j)])
                dd = qo if cg < 2 else ko
                dst = dd[t * 128:(t + 1) * 128, (cg % 2) * 512:(cg % 2 + 1) * 512]
            S.dma("sp", lambda e, j=j, dst=dst: e.dma_start(out=dst, in_=ost[j][:]),
                  reads=[("a_ost", j)], is_out=True)
    for f in range(8):
        for tb in range(4):
            j = n % 3
            n += 1
            for k in range(8):
                S.op("pe", lambda e, j=j, k=k, f=f, tb=tb: e.matmul(
                    pss[j][:], lhsT=Wb[:, k, 3 * D + f * 128:3 * D + (f + 1) * 128],
                    rhs=hnT[:, k, tb * 512:(tb + 1) * 512], start=(k == 0), stop=(k == 7)),
                    reads=[("w", 6 + f // 4)] + [("hnT", tb * 4 + x) for x in range(4)], writes=[("a_ps", j)])
            S.op("act", lambda e, j=j: e.activation(out=ost[j][:], in_=pss[j][:], func=AF.Silu),
                 reads=[("a_ps", j)], writes=[("a_ost", j)])
            S.dma("sp", lambda e, j=j, f=f, tb=tb: e.dma_start(
                out=zsT[f * 128:(f + 1) * 128, tb * 512:(tb + 1) * 512], in_=ost[j][:]),
                reads=[("a_ost", j)], is_out=True)
    return C.finish()


NEG = -30000.0


def build_attn_B():
    C = Ctx()
    S = C.S
    NTT = SEQ // 128
    qd = C.din("q", [SEQ, 128], BF16)
    kd = C.din("k", [SEQ, 128], BF16)
    vd = C.din("v", [SEQ, 128], BF16)
    idb = C.din("ident_bf", [128, 128], BF16)
    idf = C.din("ident_f", [128, 128], F32)
    trid = C.din("tri01", [128, 128], BF16)
    ead = C.din("eall", [64, 64 * 128], BF16)
    oTd = C.dout("oT", [128, SEQ], BF16)
    ident_bf = C.sb("ident_bf", [128, 128], BF16)
    ident_f = C.sb("ident_f", [128, 128], F32)
    tri = C.sb("tri", [128, 128], BF16)
    eall = C.sb("eall", [64, 64, 128], BF16)
    ones = C.sb("ones", [128, 128], BF16)
    tokst = C.sb("tokst", [128, NTT, 128], BF16)
    V = C.sb("V", [128, NTT, 128], BF16)
    kT = C.sb("kT", [128, SEQ], BF16)
    qT = C.sb("qT", [128, SEQ], BF16)
    S.dma("sp", lambda e: e.dma_start(out=ident_bf[:], in_=idb), writes=["ident_bf"])
    S.dma("sp", lambda e: e.dma_start(out=ident_f[:], in_=idf), writes=["ident_f"])
    S.dma("sp", lambda e: e.dma_start(out=tri[:], in_=trid), writes=["tri"])
    S.dma("sp", lambda e: e.dma_start(out=eall[:], in_=ead.rearrange("p (n c) -> p n c", n=64)), writes=["eall"])
    S.op("pool", lambda e: e.memset(ones[:], 1.0), writes=["ones"])
    pT = [C.ps("b_pT%d" % i, [128, 1024], BF16) for i in range(2)]

    def load_T(src, dstT, dkey):
        sv = src.rearrange("(t p) d -> p t d", p=128)
        for c4 in range(4):
            S.dma("sp", lambda e, c4=c4: e.dma_start(out=tokst[:, c4 * 32:(c4 + 1) * 32, :],
                                                     in_=sv[:, c4 * 32:(c4 + 1) * 32, :]),
                  writes=[("tokst", c4)])
        for tt in range(NTT):
            i = (tt // 8) % 2
            S.op("pe", lambda e, tt=tt, i=i: e.transpose(out=pT[i][:, (tt % 8) * 128:(tt % 8 + 1) * 128],
                                                         in_=tokst[:, tt, :], identity=ident_bf[:]),
                 reads=[("tokst", tt // 32), "ident_bf"], writes=[("b_pT", i)])
            if tt % 8 == 7:
                copy_on(S, "act" if (tt // 8) % 2 == 0 else "dve", dstT[:, (tt - 7) * 128:(tt + 1) * 128], pT[i][:],
                        reads=[("b_pT", i)], writes=[(dkey, tt // 8)])

    load_T(kd, kT, "kT")
    load_T(qd, qT, "qT")
    vv = vd.rearrange("(t p) d -> p t d", p=128)
    for c4 in range(4):
        S.dma("sp", lambda e, c4=c4: e.dma_start(out=V[:, c4 * 32:(c4 + 1) * 32, :], in_=vv[:, c4 * 32:(c4 + 1) * 32, :]),
              writes=[("V", c4)])
    km = C.sb("km", [128, 64], F32)
    kmh = C.sb("kmh", [128, 64], BF16)
    kmhf = C.sb("kmhf", [128, 64], F32)
    kml = C.sb("kml", [128, 64], BF16)
    allk = [("kT", i) for i in range(16)]
    S.op("dve", lambda e: e.tensor_reduce(out=km[:], in_=kT[:].rearrange("p (n c) -> p n c", c=256), axis=AX.X, op=ALU.add),
         reads=allk, writes=["km"])
    S.op("dve", lambda e: e.tensor_scalar(out=km[:], in0=km[:], scalar1=1.0 / 256, scalar2=None, op0=ALU.mult),
         reads=["km"], writes=["km"])
    S.op("dve", lambda e: e.tensor_copy(out=kmh[:], in_=km[:]), reads=["km"], writes=["kmh"])
    S.op("dve", lambda e: e.tensor_copy(out=kmhf[:], in_=kmh[:]), reads=["kmh"], writes=["kmhf"])
    S.op("dve", lambda e: e.tensor_tensor(out=kml[:], in0=km[:], in1=kmhf[:], op=ALU.subtract),
         reads=["km", "kmhf"], writes=["kml"])

    psm = C.ps("b_psm", [128, 512], F32)
    psS = [C.ps("b_psS%d" % i, [128, 512], F32) for i in range(3)]
    acc_o = C.ps("b_acc_o", [128, 512], F32)
    acc_s = C.ps("b_acc_s", [128, 512], F32)
    gate = [C.sb("b_gate%d" % i, [128, 64], F32) for i in range(2)]
    top8 = [C.sb("b_top%d" % i, [128, 8], F32) for i in range(2)]
    bias = [C.sb("b_bias%d" % i, [128, 64], F32) for i in range(2)]
    mbT = [C.sb("b_mbT%d" % i, [64, 512], BF16) for i in range(2)]
    PT = [C.sb("b_PT%d" % i, [128, 512], BF16) for i in range(3)]
    rs = C.sb("b_rs", [128, 512], F32)
    osb = [C.sb("b_osb%d" % i, [128, 512], BF16) for i in range(2)]
    scale = 128.0 ** -0.5
    nS = 0
    for s in range(SEQ // 512):
        mi = s % 2
        for jq in range(4):
            qt = 4 * s + jq
            own = qt // 2
            gi = qt % 2
            qk = ("qT", qt // 8)
            if own > 0:
                S.op("pool", lambda e, gi=gi: e.memset(gate[gi][:], -1e30), writes=[("b_gate", gi)])
                S.op("pe", lambda e, qt=qt: e.matmul(psm[:, 0:64], lhsT=qT[:, qt * 128:(qt + 1) * 128], rhs=kmh[:],
                                                     start=True, stop=False),
                     reads=[qk, "kmh"], writes=["b_psm"])
                S.op("pe", lambda e, qt=qt: e.matmul(psm[:, 0:64], lhsT=qT[:, qt * 128:(qt + 1) * 128], rhs=kml[:],
                                                     start=False, stop=True),
                     reads=[qk, "kml"], writes=["b_psm"])
                S.op("dve", lambda e, gi=gi, own=own: e.tensor_copy(out=gate[gi][:, 0:own], in_=psm[:, 0:own]),
                     reads=["b_psm"], writes=[("b_gate", gi)])
                S.op("dve", lambda e, gi=gi: e.max(out=top8[gi][:], in_=gate[gi][:]),
                     reads=[("b_gate", gi)], writes=[("b_top", gi)])
                S.op("dve", lambda e, gi=gi: e.tensor_scalar(out=bias[gi][:], in0=gate[gi][:], scalar1=top8[gi][:, 2:3],
                                                             scalar2=None, op0=ALU.is_ge),
                     reads=[("b_gate", gi), ("b_top", gi)], writes=[("b_bias", gi)])
                S.op("dve", lambda e, gi=gi: e.tensor_scalar(out=bias[gi][:], in0=bias[gi][:], scalar1=-1.0,
                                                             scalar2=-NEG, op0=ALU.add, op1=ALU.mult),
                     reads=[("b_bias", gi)], writes=[("b_bias", gi)])
                S.op("pool", lambda e, gi=gi, own=own: e.memset(bias[gi][:, own:own + 1], 0.0),
                     reads=[("b_bias", gi)], writes=[("b_bias", gi)])
                if own + 1 < 64:
                    S.op("pool", lambda e, gi=gi, own=own: e.memset(bias[gi][:, own + 1:64], NEG),
                         reads=[("b_bias", gi)], writes=[("b_bias", gi)])
            else:
                S.op("pool", lambda e, gi=gi: e.memset(bias[gi][:], NEG), writes=[("b_bias", gi)])
                S.op("pool", lambda e, gi=gi: e.memset(bias[gi][:, 0:1], 0.0),
                     reads=[("b_bias", gi)], writes=[("b_bias", gi)])
            S.op("pe", lambda e, gi=gi: e.transpose(out=psm[0:64, 128:256], in_=bias[gi][:], identity=ident_f[:]),
                 reads=[("b_bias", gi), "ident_f"], writes=["b_psm"])
            copy_on(S, "act", mbT[mi][:, jq * 128:(jq + 1) * 128], psm[0:64, 128:256],
                    reads=["b_psm"], writes=[("b_mbT", mi)])
        ktiles = list(range(4 * s + 4))
        qkeys = [("qT", (4 * s) // 8)]
        for idx, kt in enumerate(ktiles):
            first = idx == 0
            last = idx == len(ktiles) - 1
            pi = nS % 3
            nS += 1
            nb = kt // 2
            S.op("pe", lambda e, pi=pi, kt=kt, s=s: e.matmul(psS[pi][:], lhsT=kT[:, kt * 128:(kt + 1) * 128],
                                                             rhs=qT[:, s * 512:(s + 1) * 512], start=True, stop=False),
                 reads=[("kT", kt // 8)] + qkeys, writes=[("b_psS", pi)])
            S.op("pe", lambda e, pi=pi, nb=nb, mi=mi: e.matmul(psS[pi][:], lhsT=eall[:, nb, :], rhs=mbT[mi][:],
                                                               start=False, stop=True),
                 reads=["eall", ("b_mbT", mi)], writes=[("b_psS", pi)])
            S.op("act", lambda e, pi=pi: e.activation(out=PT[pi][:], in_=psS[pi][:], func=AF.Exp, scale=scale),
                 reads=[("b_psS", pi)], writes=[("b_PT", pi)])
            io = kt - 4 * s
            if io >= 0:
                if io in (1, 3):
                    S.op("pool", lambda e, pi=pi, io=io: e.memset(PT[pi][:, (io - 1) * 128:io * 128], 0.0),
                         reads=[("b_PT", pi)], writes=[("b_PT", pi)])
                S.op("pool", lambda e, pi=pi, io=io: e.tensor_tensor(
                    out=PT[pi][:, io * 128:(io + 1) * 128], in0=PT[pi][:, io * 128:(io + 1) * 128], in1=tri[:],
                    op=ALU.mult),
                    reads=[("b_PT", pi), "tri"], writes=[("b_PT", pi)])
            S.op("pe", lambda e, pi=pi, kt=kt, first=first, last=last: e.matmul(
                acc_o[:], lhsT=V[:, kt, :], rhs=PT[pi][:], start=first, stop=last),
                reads=[("V", kt // 32), ("b_PT", pi)], writes=["b_acc_o"])
            S.op("pe", lambda e, pi=pi, first=first, last=last: e.matmul(
                acc_s[:], lhsT=ones[:], rhs=PT[pi][:], start=first, stop=last),
                reads=["ones", ("b_PT", pi)], writes=["b_acc_s"])
        oi = s % 2
        S.op("dve", lambda e: e.reciprocal(out=rs[:], in_=acc_s[:]), reads=["b_acc_s"], writes=["b_rs"])
        S.op("dve", lambda e, oi=oi: e.tensor_tensor(out=osb[oi][:], in0=acc_o[:], in1=rs[:], op=ALU.mult),
             reads=["b_acc_o", "b_rs"], writes=[("b_osb", oi)])
        S.dma("sp", lambda e, oi=oi, s=s: e.dma_start(out=oTd[:, s * 512:(s + 1) * 512], in_=osb[oi][:]),
              reads=[("b_osb", oi)], is_out=True)
    return C.finish()


NCH = SEQ // 8
DEBUG = False
PAD = 1024


def build_ssm_B():
    C = Ctx()
    S = C.S
    uTd = C.din("uT", [128, SEQ], BF16)
    lrd = C.din("lr", [128, 4], F32)
    lid = C.din("li", [128, 4], F32)
    ldtd = C.din("ldt", [128, 4], F32)
    brd = C.din("b_re", [128, 4, 16], F32)
    bid = C.din("b_im", [128, 4, 16], F32)
    crd = C.din("c_re", [128, 4, 16], F32)
    cid = C.din("c_im", [128, 4, 16], F32)
    dd = C.din("dsk", [128, 1], F32)
    idf = C.din("ident_f", [128, 128], F32)
    yTd = C.dout("yT", [128, SEQ], BF16)

    def small(name, shape):
        return C.sb(name, shape, F32)

    lr, li, ldt = small("lr", [128, 4]), small("li", [128, 4]), small("ldt", [128, 4])
    bre, bim = small("bre", [128, 4, 16]), small("bim", [128, 4, 16])
    cre, cim = small("cre", [128, 4, 16]), small("cim", [128, 4, 16])
    dsk = small("dsk", [128, 1])
    ident_f = small("ident_f", [128, 128])
    for t_, d_, k_ in ((lr, lrd, "lr"), (li, lid, "li"), (ldt, ldtd, "ldt"), (bre, brd, "bre"), (bim, bid, "bim"),
                       (cre, crd, "cre"), (cim, cid, "cim"), (dsk, dd, "dsk"), (ident_f, idf, "ident_f")):
        S.dma("sp", lambda e, t_=t_, d_=d_: e.dma_start(out=t_[:], in_=d_), writes=[k_])
    uT = C.sb("uT", [128, SEQ], BF16)
    for c4 in range(4):
        S.dma("sp", lambda e, c4=c4: e.dma_start(out=uT[:, c4 * 4096:(c4 + 1) * 4096],
                                                 in_=uTd[:, c4 * 4096:(c4 + 1) * 4096]), writes=[("uT", c4)])

    cnt = [0]

    def T(shape=(128, 4)):
        cnt[0] += 1
        return small("pp%d" % cnt[0], list(shape)), ("pp", cnt[0])

    def tt(out, a, b, op, eng="dve"):
        (o, ok), (a_, ak), (b_, bk) = out, a, b
        S.op(eng, lambda e: e.tensor_tensor(out=o[:], in0=a_[:], in1=b_[:], op=op), reads=[ak, bk], writes=[ok])

    def ts(out, a, s1, s2, op0, op1=None, eng="dve"):
        (o, ok), (a_, ak) = out, a
        if op1 is None:
            S.op(eng, lambda e: e.tensor_scalar(out=o[:], in0=a_[:], scalar1=s1, scalar2=None, op0=op0),
                 reads=[ak], writes=[ok])
        else:
            S.op(eng, lambda e: e.tensor_scalar(out=o[:], in0=a_[:], scalar1=s1, scalar2=s2, op0=op0, op1=op1),
                 reads=[ak], writes=[ok])

    def act(out, a, func, scale=1.0, bias=0.0):
        (o, ok), (a_, ak) = out, a
        S.op("act", lambda e: e.activation(out=o[:], in_=a_[:], func=func, scale=scale, bias=bias),
             reads=[ak], writes=[ok])

    LR, LI, LDT = (lr, "lr"), (li, "li"), (ldt, "ldt")
    dt_ = T(); act(dt_, LDT, AF.Exp)
    lrdt = T(); tt(lrdt, LR, dt_, ALU.mult)
    ang = T(); tt(ang, LI, dt_, ALU.mult)
    mag = T(); act(mag, lrdt, AF.Exp)
    sn = T(); act(sn, ang, AF.Sin, scale=1.0 / 16)
    sh = T(); act(sh, ang, AF.Sin, scale=1.0 / 32)
    shh = T(); tt(shh, sh, sh, ALU.mult)
    cs = T(); ts(cs, shh, -2.0, 1.0, ALU.mult, ALU.add)
    sn0, cs0 = sn, cs
    for _ in range(4):
        c2, s2_, cc_, ss_ = T(), T(), T(), T()
        tt(cc_, cs, cs, ALU.mult)
        tt(ss_, sn, sn, ALU.mult)
        tt(c2, cc_, ss_, ALU.subtract)
        tt(s2_, cs, sn, ALU.mult)
        s3 = T(); ts(s3, s2_, 2.0, None, ALU.mult)
        cs, sn = c2, s3
    ar = T(); tt(ar, mag, cs, ALU.mult)
    ai = T(); tt(ai, mag, sn, ALU.mult)
    den = T(); t1 = T(); t2 = T()
    tt(t1, LR, LR, ALU.mult); tt(t2, LI, LI, ALU.mult); tt(den, t1, t2, ALU.add)
    rden = T()
    S.op("dve", lambda e: e.reciprocal(out=rden[0][:], in_=den[0][:]), reads=[den[1]], writes=[rden[1]])
    nr = T(); ts(nr, ar, -1.0, None, ALU.add)
    fr = T(); fi = T(); t3 = T(); t4 = T(); t5 = T(); t6 = T()
    tt(t3, nr, LR, ALU.mult); tt(t4, ai, LI, ALU.mult); tt(t5, t3, t4, ALU.add); tt(fr, t5, rden, ALU.mult)
    tt(t3, ai, LR, ALU.mult); tt(t4, nr, LI, ALU.mult); tt(t6, t3, t4, ALU.subtract); tt(fi, t6, rden, ALU.mult)
    Pre, Pim = small("Pre", [128, 9, 4]), small("Pim", [128, 9, 4])
    S.op("pool", lambda e: e.memset(Pre[:, 0, :], 1.0), writes=[("P", 0)])
    S.op("pool", lambda e: e.memset(Pim[:, 0, :], 0.0), reads=[("P", 0)], writes=[("P", 0)])
    tmpa, tmpb = T(), T()
    for m in range(1, 9):
        S.op("dve", lambda e, m=m: e.tensor_tensor(out=tmpa[0][:], in0=Pre[:, m - 1, :], in1=ar[0][:], op=ALU.mult),
             reads=[("P", m - 1), ar[1]], writes=[tmpa[1]])
        S.op("dve", lambda e, m=m: e.tensor_tensor(out=tmpb[0][:], in0=Pim[:, m - 1, :], in1=ai[0][:], op=ALU.mult),
             reads=[("P", m - 1), ai[1]], writes=[tmpb[1]])
        S.op("dve", lambda e, m=m: e.tensor_tensor(out=Pre[:, m, :], in0=tmpa[0][:], in1=tmpb[0][:], op=ALU.subtract),
             reads=[tmpa[1], tmpb[1]], writes=[("P", m)])
        S.op("dve", lambda e, m=m: e.tensor_tensor(out=tmpa[0][:], in0=Pre[:, m - 1, :], in1=ai[0][:], op=ALU.mult),
             reads=[("P", m - 1), ai[1]], writes=[tmpa[1]])
        S.op("dve", lambda e, m=m: e.tensor_tensor(out=tmpb[0][:], in0=Pim[:, m - 1, :], in1=ar[0][:], op=ALU.mult),
             reads=[("P", m - 1), ar[1]], writes=[tmpb[1]])
        S.op("dve", lambda e, m=m: e.tensor_tensor(out=Pim[:, m, :], in0=tmpa[0][:], in1=tmpb[0][:], op=ALU.add),
             reads=[tmpa[1], tmpb[1], ("P", m)], writes=[("P", m)])
    Dre, Dim, nDim = small("Dre", [128, 11, 4]), small("Dim", [128, 11, 4]), small("nDim", [128, 11, 4])
    S.op("dve", lambda e: e.tensor_copy(out=Dre[:, 0, :], in_=Pre[:, 8, :]), reads=[("P", 8)], writes=[("Dd", 0)])
    S.op("dve", lambda e: e.tensor_copy(out=Dim[:, 0, :], in_=Pim[:, 8, :]), reads=[("P", 8), ("Dd", 0)], writes=[("Dd", 0)])
    for k in range(1, 11):
        S.op("dve", lambda e, k=k: e.tensor_tensor(out=tmpa[0][:], in0=Dre[:, k - 1, :], in1=Dre[:, k - 1, :], op=ALU.mult),
             reads=[("Dd", k - 1)], writes=[tmpa[1]])
        S.op("dve", lambda e, k=k: e.tensor_tensor(out=tmpb[0][:], in0=Dim[:, k - 1, :], in1=Dim[:, k - 1, :], op=ALU.mult),
             reads=[("Dd", k - 1)], writes=[tmpb[1]])
        S.op("dve", lambda e, k=k: e.tensor_tensor(out=Dre[:, k, :], in0=tmpa[0][:], in1=tmpb[0][:], op=ALU.subtract),
             reads=[tmpa[1], tmpb[1]], writes=[("Dd", k)])
        S.op("dve", lambda e, k=k: e.tensor_tensor(out=tmpa[0][:], in0=Dre[:, k - 1, :], in1=Dim[:, k - 1, :], op=ALU.mult),
             reads=[("Dd", k - 1)], writes=[tmpa[1]])
        S.op("dve", lambda e, k=k: e.tensor_scalar(out=Dim[:, k, :], in0=tmpa[0][:], scalar1=2.0, scalar2=None, op0=ALU.mult),
             reads=[tmpa[1], ("Dd", k)], writes=[("Dd", k)])
    allD = [("Dd", k) for k in range(11)]
    S.op("dve", lambda e: e.tensor_scalar(out=nDim[:], in0=Dim[:], scalar1=-1.0, scalar2=None, op0=ALU.mult),
         reads=allD, writes=["nDim"])
    allP = [("P", m) for m in range(9)]

    M2b = C.sb("M2b", [128, 4, 2, 8, 128], BF16)
    M3b = C.sb("M3b", [128, 4, 2, 8, 128], BF16)
    Kacc = C.sb("Kacc", [128, 8, 128], F32)
    Kb = C.sb("Kb", [128, 8, 128], BF16)
    WBpad = C.sb("WBpad", [128, 2, 8, 128], F32)
    CPpad = C.sb("CPpad", [128, 2, 9, 128], F32)
    Bbr, Bbi = small("Bbr", [128, 16]), small("Bbi", [128, 16])
    wa = [small("wa%d" % i, [128, 9, 16]) for i in range(4)]
    WBc = [small("WBc%d" % i, [128, 8, 16]) for i in range(2)]
    CPc = [small("CPc%d" % i, [128, 9, 16]) for i in range(2)]
    psA = C.ps("psA", [128, 512], F32)
    psB = C.ps("psB", [128, 512], F32)

    def bc_mid(ap2, n):
        return ap2.unsqueeze(1).to_broadcast([128, n, 16])

    def bc_last(ap2, n):
        return ap2.unsqueeze(2).to_broadcast([128, n, 16])

    for q in range(4):
        S.op("dve", lambda e, q=q: e.tensor_scalar(out=Bbr[:], in0=bre[:, q, :], scalar1=fr[0][:, q:q + 1], scalar2=None, op0=ALU.mult),
             reads=["bre", fr[1]], writes=["Bbr"])
        S.op("dve", lambda e, q=q: e.tensor_scalar(out=wa[0][:, 0, :], in0=bim[:, q, :], scalar1=fi[0][:, q:q + 1], scalar2=None, op0=ALU.mult),
             reads=["bim", fi[1]], writes=[("wa", 0)])
        S.op("dve", lambda e: e.tensor_tensor(out=Bbr[:], in0=Bbr[:], in1=wa[0][:, 0, :], op=ALU.subtract),
             reads=["Bbr", ("wa", 0)], writes=["Bbr"])
        S.op("dve", lambda e, q=q: e.tensor_scalar(out=Bbi[:], in0=bim[:, q, :], scalar1=fr[0][:, q:q + 1], scalar2=None, op0=ALU.mult),
             reads=["bim", fr[1]], writes=["Bbi"])
        S.op("dve", lambda e, q=q: e.tensor_scalar(out=wa[0][:, 0, :], in0=bre[:, q, :], scalar1=fi[0][:, q:q + 1], scalar2=None, op0=ALU.mult),
             reads=["bre", fi[1]], writes=[("wa", 0)])
        S.op("dve", lambda e: e.tensor_tensor(out=Bbi[:], in0=Bbi[:], in1=wa[0][:, 0, :], op=ALU.add),
             reads=["Bbi", ("wa", 0)], writes=["Bbi"])

        def cprod(n, xr, xi, xkeys, outs, okeys, neg_im, q=q):
            pr = Pre[:, 0:n, q]
            pi_ = Pim[:, 0:n, q]
            S.op("dve", lambda e: e.tensor_tensor(out=wa[0][:, 0:n, :], in0=bc_mid(xr, n), in1=bc_last(pr, n), op=ALU.mult),
                 reads=xkeys + allP, writes=[("wa", 0)])
            S.op("pool", lambda e: e.tensor_tensor(out=wa[1][:, 0:n, :], in0=bc_mid(xi, n), in1=bc_last(pi_, n), op=ALU.mult),
                 reads=xkeys + allP, writes=[("wa", 1)])
            S.op("dve", lambda e: e.tensor_tensor(out=wa[2][:, 0:n, :], in0=bc_mid(xr, n), in1=bc_last(pi_, n), op=ALU.mult),
                 reads=xkeys + allP, writes=[("wa", 2)])
            S.op("pool", lambda e: e.tensor_tensor(out=wa[3][:, 0:n, :], in0=bc_mid(xi, n), in1=bc_last(pr, n), op=ALU.mult),
                 reads=xkeys + allP, writes=[("wa", 3)])
            S.op("dve", lambda e: e.tensor_tensor(out=outs[0][:, 0:n, :], in0=wa[0][:, 0:n, :], in1=wa[1][:, 0:n, :], op=ALU.subtract),
                 reads=[("wa", 0), ("wa", 1)], writes=[okeys[0]])
            if neg_im:
                S.op("dve", lambda e: e.scalar_tensor_tensor(out=outs[1][:, 0:n, :], in0=wa[2][:, 0:n, :], scalar=-1.0,
                                                             in1=wa[3][:, 0:n, :], op0=ALU.mult, op1=ALU.subtract),
                     reads=[("wa", 2), ("wa", 3)], writes=[okeys[1]])
            else:
                S.op("dve", lambda e: e.tensor_tensor(out=outs[1][:, 0:n, :], in0=wa[2][:, 0:n, :], in1=wa[3][:, 0:n, :], op=ALU.add),
                     reads=[("wa", 2), ("wa", 3)], writes=[okeys[1]])

        cprod(8, Bbr[:], Bbi[:], ["Bbr", "Bbi"], WBc, [("WBc", 0), ("WBc", 1)], False)
        cprod(9, cre[:, q, :], cim[:, q, :], ["cre", "cim"], CPc, [("CPc", 0), ("CPc", 1)], True)
        S.op("pool", lambda e: e.memset(WBpad[:], 0.0), writes=["WBpad"])
        S.op("pool", lambda e: e.memset(CPpad[:], 0.0), writes=["CPpad"])
        for ri in range(2):
            for e_ in range(2):
                c0 = (2 * q + e_) * 16
                S.op("dve", lambda e, ri=ri, e_=e_, c0=c0: e.tensor_copy(
                    out=WBpad[e_ * 64:(e_ + 1) * 64, ri, :, c0:c0 + 16], in_=WBc[ri][e_ * 64:(e_ + 1) * 64, :, :]),
                    reads=[("WBc", ri), "WBpad"], writes=["WBpad"])
                S.op("pool", lambda e, ri=ri, e_=e_, c0=c0: e.tensor_copy(
                    out=CPpad[e_ * 64:(e_ + 1) * 64, ri, :, c0:c0 + 16], in_=CPc[ri][e_ * 64:(e_ + 1) * 64, :, :]),
                    reads=[("CPc", ri), "CPpad"], writes=["CPpad"])
        for ri in range(2):
            for half in range(2):
                ps = psA if (ri * 2 + half) % 2 == 0 else psB
                pk = "psA" if (ri * 2 + half) % 2 == 0 else "psB"
                for x in range(4):
                    tau = half * 4 + x
                    S.op("pe", lambda e, ps=ps, ri=ri, tau=tau, x=x: e.transpose(
                        out=ps[:, x * 128:(x + 1) * 128], in_=WBpad[:, ri, tau, :], identity=ident_f[:]),
                        reads=["WBpad", "ident_f"], writes=[pk])
                for x in range(4):
                    tau = half * 4 + x
                    copy_on(S, "act", M2b[:, q, ri, 7 - tau, :], ps[:, x * 128:(x + 1) * 128],
                            reads=[pk], writes=[("M2b", q)])
        for half in range(2):
            ps = psA if half == 0 else psB
            pk = "psA" if half == 0 else "psB"
            for x in range(4):
                tau = half * 4 + x
                S.op("pe", lambda e, ps=ps, tau=tau, x=x: e.matmul(ps[:, x * 128:(x + 1) * 128], lhsT=WBpad[:, 0, tau, :],
                                                                   rhs=CPpad[:, 0, 0, :], start=True, stop=False),
                     reads=["WBpad", "CPpad"], writes=[pk])
                S.op("pe", lambda e, ps=ps, tau=tau, x=x: e.matmul(ps[:, x * 128:(x + 1) * 128], lhsT=WBpad[:, 1, tau, :],
                                                                   rhs=CPpad[:, 1, 0, :], start=False, stop=True),
                     reads=["WBpad", "CPpad"], writes=[pk])
            kv = Kacc[:, half * 4:(half + 1) * 4, :].rearrange("p a b -> p (a b)")
            if q == 0:
                S.op("dve", lambda e, ps=ps, kv=kv: e.tensor_copy(out=kv, in_=ps[:]), reads=[pk], writes=[("Kacc", half)])
            else:
                S.op("dve", lambda e, ps=ps, kv=kv: e.tensor_tensor(out=kv, in0=ps[:], in1=kv, op=ALU.add),
                     reads=[pk, ("Kacc", half)], writes=[("Kacc", half)])
        for ri in range(2):
            copy_on(S, "act", M3b[:, q, ri, :, :], CPpad[:, ri, 1:9, :], reads=["CPpad"], writes=[("M3b", q)])
    S.op("dve", lambda e: e.tensor_copy(out=Kb[:], in_=Kacc[:]), reads=[("Kacc", 0), ("Kacc", 1)], writes=["Kb"])

    Sbuf = [[C.sb("S%d%d" % (a, r), [128, PAD + NCH], F32) for r in range(2)] for a in range(2)]
    Sb = C.sb("Sb", [128, 4, 2, NCH + 1], BF16)
    for a in range(2):
        for r in range(2):
            S.op("pool", lambda e, a=a, r=r: e.memset(Sbuf[a][r][:, 0:PAD], 0.0), writes=[("Spad", a, r)])
    S.op("pool", lambda e: e.memset(Sb[:, :, :, 0:1], 0.0), writes=["Sb0"])
    u3 = uT[:].rearrange("p (k j) -> p k j", j=8)
    psX = [psA, psB]
    nx = 0
    for q in range(4):
        for ri in range(2):
            for pc in range(4):
                ps = psX[nx % 2]
                pk = "psA" if nx % 2 == 0 else "psB"
                nx += 1
                for jp in range(8):
                    S.op("pe", lambda e, ps=ps, q=q, ri=ri, jp=jp, pc=pc: e.matmul(
                        ps[:], lhsT=M2b[:, q, ri, jp, :], rhs=u3[:, pc * 512:(pc + 1) * 512, jp],
                        start=(jp == 0), stop=(jp == 7)),
                        reads=[("M2b", q), ("uT", pc)], writes=[pk])
                copy_on(S, "act", Sbuf[0][ri][:, PAD + pc * 512:PAD + (pc + 1) * 512], ps[:],
                        reads=[pk], writes=[("S", 0, ri)])
        cur, nxt = 0, 1
        for k in range(11):
            m = 1 << k
            cr, ci = Sbuf[cur][0], Sbuf[cur][1]
            nr_, ni_ = Sbuf[nxt][0], Sbuf[nxt][1]
            rd = [("S", cur, 0), ("S", cur, 1), ("Spad", cur, 0), ("Spad", cur, 1)] + allD + ["nDim"]
            S.op("dve", lambda e, cr=cr, nr_=nr_, k=k, m=m, q=q: e.scalar_tensor_tensor(
                out=nr_[:, PAD:], in0=cr[:, PAD - m:PAD - m + NCH], scalar=Dre[:, k, q:q + 1], in1=cr[:, PAD:],
                op0=ALU.mult, op1=ALU.add), reads=rd, writes=[("S", nxt, 0)])
            S.op("dve", lambda e, ci=ci, nr_=nr_, k=k, m=m, q=q: e.scalar_tensor_tensor(
                out=nr_[:, PAD:], in0=ci[:, PAD - m:PAD - m + NCH], scalar=nDim[:, k, q:q + 1], in1=nr_[:, PAD:],
                op0=ALU.mult, op1=ALU.add), reads=rd + [("S", nxt, 0)], writes=[("S", nxt, 0)])
            S.op("dve", lambda e, ci=ci, ni_=ni_, k=k, m=m, q=q: e.scalar_tensor_tensor(
                out=ni_[:, PAD:], in0=ci[:, PAD - m:PAD - m + NCH], scalar=Dre[:, k, q:q + 1], in1=ci[:, PAD:],
                op0=ALU.mult, op1=ALU.add), reads=rd, writes=[("S", nxt, 1)])
            S.op("dve", lambda e, cr=cr, ni_=ni_, k=k, m=m, q=q: e.scalar_tensor_tensor(
                out=ni_[:, PAD:], in0=cr[:, PAD - m:PAD - m + NCH], scalar=Dim[:, k, q:q + 1], in1=ni_[:, PAD:],
                op0=ALU.mult, op1=ALU.add), reads=rd + [("S", nxt, 1)], writes=[("S", nxt, 1)])
            cur, nxt = nxt, cur
        for ri in range(2):
            copy_on(S, "act", Sb[:, q, ri, 1:NCH + 1], Sbuf[cur][ri][:, PAD:], reads=[("S", cur, ri)],
                    writes=[("Sb", q)])
    psY = [C.ps("psY%d" % i, [128, 512], F32) for i in range(3)]
    ybuf = [C.sb("ybuf%d" % i, [128, 512, 8], BF16) for i in range(1)]
    vv_ = [C.sb("g_v%d" % i, [128, 512], F32) for i in range(1)]
    w1 = [C.sb("g_w%d" % i, [128, 512], F32) for i in range(1)]
    w2 = [C.sb("g_x%d" % i, [128, 512], F32) for i in range(1)]
    sgt = [C.sb("g_s%d" % i, [128, 512], F32) for i in range(1)]
    allSb = [("Sb", q) for q in range(4)] + ["Sb0"]
    ny = 0
    for pc in range(4):
        yb = ybuf[0]
        for j in range(8):
            pi = ny % 3
            gi = 0
            ny += 1
            ps = psY[pi]
            pk = ("psY", pi)
            for jp in range(j + 1):
                S.op("pe", lambda e, ps=ps, j=j, jp=jp, pc=pc: e.matmul(
                    ps[:], lhsT=Kb[:, j - jp, :], rhs=u3[:, pc * 512:(pc + 1) * 512, jp], start=(jp == 0), stop=False),
                    reads=["Kb", ("uT", pc)], writes=[pk])
            for q in range(4):
                for ri in range(2):
                    S.op("pe", lambda e, ps=ps, q=q, ri=ri, j=j, pc=pc: e.matmul(
                        ps[:], lhsT=M3b[:, q, ri, j, :], rhs=Sb[:, q, ri, pc * 512:(pc + 1) * 512],
                        start=False, stop=(q == 3 and ri == 1)),
                        reads=[("M3b", q)] + allSb, writes=[pk])
            S.op("dve", lambda e, ps=ps, gi=gi, j=j, pc=pc: e.scalar_tensor_tensor(
                out=vv_[gi][:], in0=u3[:, pc * 512:(pc + 1) * 512, j], scalar=dsk[:, 0:1], in1=ps[:],
                op0=ALU.mult, op1=ALU.add), reads=[pk, ("uT", pc), "dsk"], writes=[("g_v", gi)])
            S.op("act", lambda e, gi=gi: e.activation(out=w1[gi][:], in_=vv_[gi][:], func=AF.Square),
                 reads=[("g_v", gi)], writes=[("g_w", gi)])
            S.op("pool", lambda e, gi=gi: e.tensor_scalar(out=w1[gi][:], in0=w1[gi][:], scalar1=0.044715, scalar2=1.0,
                                                          op0=ALU.mult, op1=ALU.add),
                 reads=[("g_w", gi)], writes=[("g_w", gi)])
            S.op("pool", lambda e, gi=gi: e.tensor_tensor(out=w2[gi][:], in0=w1[gi][:], in1=vv_[gi][:], op=ALU.mult),
                 reads=[("g_w", gi), ("g_v", gi)], writes=[("g_x", gi)])
            S.op("act", lambda e, gi=gi: e.activation(out=sgt[gi][:], in_=w2[gi][:], func=AF.Sigmoid,
                                                      scale=1.5957691216057308),
                 reads=[("g_x", gi)], writes=[("g_s", gi)])
            S.op("dve", lambda e, gi=gi, yb=yb, j=j: e.tensor_tensor(out=yb[:, :, j], in0=vv_[gi][:], in1=sgt[gi][:],
                                                                    op=ALU.mult),
                 reads=[("g_v", gi), ("g_s", gi)], writes=[("ybuf", 0, j)])
        S.dma("sp", lambda e, yb=yb, pc=pc: e.dma_start(out=yTd[:, pc * 4096:(pc + 1) * 4096],
                                                        in_=yb[:].rearrange("p k j -> p (k j)")),
              reads=[("ybuf", 0, j) for j in range(8)], is_out=True)
    if DEBUG:
        dbg = C.dout("dbg", [128, 4096], F32)
        items = [(fr, 4), (fi, 4), (ar, 4), (ai, 4)]
        off = 200
        for (t_, k_), n_ in [(dt_, 4), (mag, 4), (sn0, 4), (cs0, 4), (ang, 4), (lrdt, 4), (LR, 4), (LI, 4), (LDT, 4)]:
            S.dma("sp", lambda e, t_=t_, off=off, n_=n_: e.dma_start(out=dbg[:, off:off + n_], in_=t_[:]), reads=[k_], is_out=True)
            off += n_
        off = 0
        for (t_, k_), n_ in items:
            S.dma("sp", lambda e, t_=t_, off=off, n_=n_: e.dma_start(out=dbg[:, off:off + n_], in_=t_[:]), reads=[k_], is_out=True)
            off += n_
        S.dma("sp", lambda e: e.dma_start(out=dbg[:, 16:52], in_=Pre[:].rearrange("p a b -> p (a b)")), reads=allP, is_out=True)
        S.dma("sp", lambda e: e.dma_start(out=dbg[:, 52:88], in_=Pim[:].rearrange("p a b -> p (a b)")), reads=allP, is_out=True)
        S.dma("sp", lambda e: e.dma_start(out=dbg[:, 88:132], in_=Dre[:].rearrange("p a b -> p (a b)")), reads=allD, is_out=True)
        S.dma("sp", lambda e: e.dma_start(out=dbg[:, 132:176], in_=Dim[:].rearrange("p a b -> p (a b)")), reads=allD, is_out=True)
        S.dma("sp", lambda e: e.dma_start(out=dbg[:, 1024:2048], in_=Kacc[:].rearrange("p a b -> p (a b)")), reads=["Kb"], is_out=True)
        S.dma("sp", lambda e: e.dma_start(out=dbg[:, 2048:4096], in_=Sbuf[1][0][:, PAD:]), reads=[("Sb", 3)], is_out=True)
    return C.finish()


_PROGS = {}


def _prog(name):
    if name not in _PROGS:
        _PROGS[name] = {"ssm_A": build_ssm_A, "ssm_B": build_ssm_B, "ssm_C": build_ssm_C,
                        "attn_A": build_attn_A, "attn_B": build_attn_B, "attn_C": build_attn_C}[name]()
    return _PROGS[name]


def _run(name, in_maps):
    res = run_bass_kernel_spmd(_prog(name), in_maps, core_ids=list(range(NCORES)))
    return res.results


def _consts():
    ident_f = np.eye(128, dtype=np.float32)
    ident_bf = ident_f.astype(NPBF)
    tri = (np.arange(128)[:, None] <= np.arange(128)[None, :]).astype(np.float32).astype(NPBF)
    eall = np.zeros((64, 64, 128), np.float32)
    for n in range(64):
        eall[n, n, :] = 1.0
    eall = eall.reshape(64, 64 * 128).astype(NPBF)
    half = 8
    inv_freq = (np.float32(ROPE_THETA) ** (-(np.arange(half, dtype=np.float32) * np.float32(2.0)) / np.float32(32))).astype(np.float32)
    inv_freq = (np.float32(ROPE_THETA) ** (-(np.arange(16, dtype=np.float32) * np.float32(2.0)) / np.float32(32))).astype(np.float32)
    ang = np.arange(SEQ, dtype=np.float32)[:, None] * inv_freq[None, :]
    cos4 = np.tile(np.cos(ang).astype(np.float32), (1, 4))
    sin4 = np.tile(np.sin(ang).astype(np.float32), (1, 4))
    return ident_f, ident_bf, tri, eall, cos4, sin4


def _rep(v, n=128):
    return np.ascontiguousarray(np.broadcast_to(np.asarray(v, np.float32)[None, :], (n, v.shape[0])))


def ssm_layer(h_sh, g, w_in, a_re, a_im, log_dt, b_re, b_im, c_re, c_im, dsk, w_glu, b_glu, w_out, consts):
    ident_f, ident_bf = consts[0], consts[1]
    g_rep = _rep(g)
    resA = _run("ssm_A", [{"h": h_sh[r], "g_rep": g_rep, "w_in": w_in, "ident_bf": ident_bf} for r in range(NCORES)])
    inB = []
    for r in range(NCORES):
        uT = np.concatenate([resA[c]["uT"][r * 128:(r + 1) * 128, :] for c in range(NCORES)], axis=1)
        gs = slice(r * 8, (r + 1) * 8)

        def lay(x):
            return np.ascontiguousarray(x[gs].reshape(4, 2, 64).transpose(1, 2, 0).reshape(128, 4))

        def layb(x):
            return np.ascontiguousarray(x[gs].reshape(4, 2, 64, 16).transpose(1, 2, 0, 3).reshape(128, 4, 16))

        def layc(x):
            return np.ascontiguousarray(x[gs].reshape(4, 2, 16, 64).transpose(1, 3, 0, 2).reshape(128, 4, 16))
        ldt = np.broadcast_to(log_dt[:, None], (64, 64))
        inB.append({"uT": np.ascontiguousarray(uT), "lr": lay(a_re), "li": lay(a_im), "ldt": lay(ldt),
                    "b_re": layb(b_re), "b_im": layb(b_im), "c_re": layc(c_re), "c_im": layc(c_im),
                    "dsk": np.ascontiguousarray(dsk[r * 128:(r + 1) * 128, None]), "ident_f": ident_f})
    resB = _run("ssm_B", inB)
    inC = []
    bgl = np.ascontiguousarray(b_glu.reshape(8, 128).T)
    for c in range(NCORES):
        yT = np.concatenate([resB[r]["yT"][:, c * TOK:(c + 1) * TOK] for r in range(NCORES)], axis=0)
        inC.append({"h": h_sh[c], "yT": np.ascontiguousarray(yT), "zsT": resA[c]["zsT"], "w_glu": w_glu,
                    "b_glu": bgl, "w_out": w_out})
    resC = _run("ssm_C", inC)
    return [resC[c]["hout"] for c in range(NCORES)]


def attn_layer(h_sh, g, w_in, qg, kg, w_out, consts):
    ident_f, ident_bf, tri, eall, cos4, sin4 = consts
    g_rep = _rep(g)
    resA = _run("attn_A", [{"h": h_sh[r], "g_rep": g_rep, "w_in": w_in, "ident_bf": ident_bf,
                            "qg_rep": _rep(qg), "kg_rep": _rep(kg),
                            "cos4": np.ascontiguousarray(cos4[r * TOK:(r + 1) * TOK]),
                            "sin4": np.ascontiguousarray(sin4[r * TOK:(r + 1) * TOK])} for r in range(NCORES)])
    inB = []
    for hd in range(NCORES):
        cs = slice(hd * 128, (hd + 1) * 128)
        inB.append({"q": np.ascontiguousarray(np.concatenate([resA[c]["q"][:, cs] for c in range(NCORES)], axis=0)),
                    "k": np.ascontiguousarray(np.concatenate([resA[c]["k"][:, cs] for c in range(NCORES)], axis=0)),
                    "v": np.ascontiguousarray(np.concatenate([resA[c]["v"][:, cs] for c in range(NCORES)], axis=0)),
                    "ident_bf": ident_bf, "ident_f": ident_f, "tri01": tri, "eall": eall})
    resB = _run("attn_B", inB)
    inC = []
    for c in range(NCORES):
        oT = np.concatenate([resB[hd]["oT"][:, c * TOK:(c + 1) * TOK] for hd in range(NCORES)], axis=0)
        inC.append({"h": h_sh[c], "oT": np.ascontiguousarray(oT), "zsT": resA[c]["zsT"], "w_out": w_out})
    resC = _run("attn_C", inC)
    return [resC[c]["hout"] for c in range(NCORES)]


def kernel(x, norm_g, ssm_w_in, ssm_a_re, ssm_a_im, ssm_log_dt, ssm_b_re, ssm_b_im, ssm_c_re, ssm_c_im,
           ssm_d, ssm_w_glu, ssm_b_glu, ssm_w_out, attn_w_in, attn_q_gain, attn_k_gain, attn_w_out):
    f = lambda a: np.ascontiguousarray(np.asarray(a, dtype=np.float32))
    x = f(x)
    consts = _consts()
    h_sh = [np.ascontiguousarray(x[0, r * TOK:(r + 1) * TOK, :]) for r in range(NCORES)]
    for i in range(4):
        j = i // 2
        if i % 2 == 0:
            h_sh = ssm_layer(h_sh, f(norm_g)[i], f(ssm_w_in)[j], f(ssm_a_re)[j], f(ssm_a_im)[j], f(ssm_log_dt)[j],
                             f(ssm_b_re)[j], f(ssm_b_im)[j], f(ssm_c_re)[j], f(ssm_c_im)[j], f(ssm_d)[j],
                             f(ssm_w_glu)[j], f(ssm_b_glu)[j], f(ssm_w_out)[j], consts)
        else:
            h_sh = attn_layer(h_sh, f(norm_g)[i], f(attn_w_in)[j], f(attn_q_gain)[j], f(attn_k_gain)[j],
                              f(attn_w_out)[j], consts)
    return np.concatenate(h_sh, axis=0)[None].astype(np.float32)
```
